# Optimizing a Trainium2 kernel written in Bass

```python
import math
import jax, jax.numpy as jnp
from jax import lax
import numpy as np

D_MODEL = 1024
BATCH = 8
SEQ = 4096
DEPTH = 4
DEC_BATCH = 32
DEC_SEQ = 64
PAST_LEN = 1024

CHUNK = 64
Q_BLOCK = 128
HEAD_DIM = 64
H_FOX = 8
H_DIFF = 4
H_DSA = 8
H_IDX = 8
D_IDX = 64
TOPK_MAX = 256
ROPE_THETA = 10000.0
D_FF = ((8 * D_MODEL + 3 * 256 - 1) // (3 * 256)) * 256
W_FOX = H_FOX * HEAD_DIM
W_DIFF = H_DIFF * 2 * HEAD_DIM
W_DSA = H_DSA * HEAD_DIM
N_BRANCH = 3
SPLIT_SIZES = (W_FOX, W_FOX, W_FOX, H_FOX, W_DIFF, W_DIFF, W_DIFF, W_DSA, W_DSA, W_DSA, H_IDX * D_IDX, D_IDX, H_IDX, N_BRANCH * D_MODEL)
SPLIT_OFFSETS = tuple(sum(SPLIT_SIZES[:i + 1]) for i in range(len(SPLIT_SIZES) - 1))
N_IN = sum(SPLIT_SIZES)
LN_EPS = 1e-5
DEEPNORM_ALPHA = (2 * DEPTH) ** 0.25
DEEPNORM_BETA = (8 * DEPTH) ** -0.25
F32 = jnp.float32

kernel_name = 'hybrid_streaming_fox_diff_dsa_step'


def _layernorm(x, g, b):
    xf = x.astype(F32)
    mu = jnp.mean(xf, axis=-1, keepdims=True)
    var = jnp.mean(jnp.square(xf - mu), axis=-1, keepdims=True)
    return ((xf - mu) * lax.rsqrt(var + LN_EPS) * g.astype(F32) + b.astype(F32)).astype(x.dtype)


def _rmsnorm(x, g):
    xf = x.astype(F32)
    out = xf * lax.rsqrt(jnp.mean(jnp.square(xf), axis=-1, keepdims=True) + LN_EPS) * g.astype(F32)
    return out.astype(x.dtype)


def _rope(x, pos):
    half = x.shape[-1] // 2
    inv = ROPE_THETA ** (-jnp.arange(half, dtype=F32) / half)
    ang = pos.astype(F32)[:, None] * inv[None, :]
    shape = (1, x.shape[1]) + (1,) * (x.ndim - 3) + (half,)
    cos = jnp.cos(ang).reshape(shape)
    sin = jnp.sin(ang).reshape(shape)
    xf = x.astype(F32)
    x1, x2 = xf[..., :half], xf[..., half:]
    return jnp.concatenate([x1 * cos - x2 * sin, x2 * cos + x1 * sin], axis=-1).astype(x.dtype)


def _sweep(block_fn, n_q):
    qb = Q_BLOCK if n_q % Q_BLOCK == 0 else n_q
    out = lax.map(lambda i: block_fn(i * qb, qb), jnp.arange(n_q // qb, dtype=jnp.int32))
    out = jnp.moveaxis(out, 0, 1)
    return out.reshape((out.shape[0], n_q) + out.shape[3:])


def _fox_block(start, qb, q, k, v, cum_q, cum_k, qpos, kpos):
    qs = lax.dynamic_slice_in_dim(q, start, qb, axis=1)
    cq = lax.dynamic_slice_in_dim(cum_q, start, qb, axis=1)
    qp = lax.dynamic_slice_in_dim(qpos, start, qb, axis=0)
    s = jnp.einsum('bqhd,bkhd->bhqk', qs, k).astype(F32) * (HEAD_DIM ** -0.5)
    s = s + jnp.swapaxes(cq, 1, 2)[..., None] - jnp.swapaxes(cum_k, 1, 2)[:, :, None, :]
    mask = kpos[None, :] <= qp[:, None]
    p = jax.nn.softmax(jnp.where(mask[None, None], s, -jnp.inf), axis=-1).astype(v.dtype)
    return jnp.einsum('bhqk,bkhd->bqhd', p, v)


def _diff_block(start, qb, q, k, v, lam, qpos, kpos):
    qs = lax.dynamic_slice_in_dim(q, start, qb, axis=1)
    qp = lax.dynamic_slice_in_dim(qpos, start, qb, axis=0)
    s = jnp.einsum('bqhcd,bkhcd->bhcqk', qs, k).astype(F32) * (HEAD_DIM ** -0.5)
    mask = (kpos // CHUNK)[None, :] <= (qp // CHUNK)[:, None]
    p = jax.nn.softmax(jnp.where(mask[None, None, None], s, -jnp.inf), axis=-1)
    a = (p[:, :, 0] - lam * p[:, :, 1]).astype(v.dtype)
    return jnp.einsum('bhqk,bkhe->bqhe', a, v)


def _dsa_block(start, qb, q, k, v, qi, ki, wi, qpos, kpos, k_sel):
    qs = lax.dynamic_slice_in_dim(q, start, qb, axis=1)
    qis = lax.dynamic_slice_in_dim(qi, start, qb, axis=1)
    wis = lax.dynamic_slice_in_dim(wi, start, qb, axis=1)
    qp = lax.dynamic_slice_in_dim(qpos, start, qb, axis=0)
    idx = jnp.einsum('bqhd,bkd->bqhk', qis, ki).astype(F32) * (D_IDX ** -0.5)
    score = jnp.einsum('bqhk,bqh->bqk', jax.nn.relu(idx), wis.astype(F32)) * (H_IDX ** -0.5)
    adm = (kpos // CHUNK)[None, :] <= (qp // CHUNK)[:, None]
    score = jnp.where(adm[None], score, -jnp.inf)
    _, top_idx = lax.top_k(score, k_sel)
    valid = (kpos[top_idx] // CHUNK) <= (qp // CHUNK)[None, :, None]
    gather = jax.vmap(lambda rows, ids: rows[ids])
    k_g = gather(k, top_idx)
    v_g = gather(v, top_idx)
    s = jnp.einsum('bqhd,bqkhd->bhqk', qs, k_g).astype(F32) * (HEAD_DIM ** -0.5)
    p = jax.nn.softmax(jnp.where(valid[:, None], s, -jnp.inf), axis=-1).astype(v.dtype)
    return jnp.einsum('bhqk,bqkhd->bqhd', p, v_g)


def _empty_past(bsz, dtype):
    z = lambda *s: jnp.zeros((bsz, 0) + s, dtype)
    return (z(H_FOX, HEAD_DIM), z(H_FOX, HEAD_DIM), z(H_FOX), z(H_DIFF, 2, HEAD_DIM), z(H_DIFF, 2 * HEAD_DIM), z(H_DSA, HEAD_DIM), z(H_DSA, HEAD_DIM), z(D_IDX))


def _layer(x, qpos, past, lp, layer_idx):
    pa_k, pa_v, pa_f, pb_k, pb_v, pc_k, pc_v, pc_i = past
    (w_in, b_fgate, b_gate, lam_q1, lam_k1, lam_q2, lam_k2, diff_g, w_br_a, w_br_b, w_br_c,
     w_o, ln1_g, ln1_b, ln2_g, ln2_b, w_ff1, w_ff3, w_ff2) = lp
    bsz, n_new, _ = x.shape
    n_keys = pa_k.shape[1] + n_new
    kpos = jnp.arange(n_keys, dtype=jnp.int32)
    (aq, ak, av, af, bq, bk, bv, cq, ck, cv, iq, ik, iw, gl) = jnp.split(x @ w_in, SPLIT_OFFSETS, axis=-1)

    aq = aq.reshape(bsz, n_new, H_FOX, HEAD_DIM)
    ak = ak.reshape(bsz, n_new, H_FOX, HEAD_DIM)
    av = av.reshape(bsz, n_new, H_FOX, HEAD_DIM)
    logf = jax.nn.log_sigmoid(af.astype(F32) + b_fgate.astype(F32))
    a_keys = jnp.concatenate([pa_k, ak], axis=1)
    a_vals = jnp.concatenate([pa_v, av], axis=1)
    cum = jnp.cumsum(jnp.concatenate([pa_f.astype(F32), logf], axis=1), axis=1)
    cum_q = cum[:, n_keys - n_new:]
    ya = _sweep(lambda s, n: _fox_block(s, n, aq, a_keys, a_vals, cum_q, cum, qpos, kpos), n_new)

    bq = _rope(bq.reshape(bsz, n_new, H_DIFF, 2, HEAD_DIM), qpos)
    bk = _rope(bk.reshape(bsz, n_new, H_DIFF, 2, HEAD_DIM), qpos)
    bv = bv.reshape(bsz, n_new, H_DIFF, 2 * HEAD_DIM)
    b_keys = jnp.concatenate([pb_k, bk], axis=1)
    b_vals = jnp.concatenate([pb_v, bv], axis=1)
    lam_init = 0.8 - 0.6 * math.exp(-0.3 * layer_idx)
    lam = (jnp.exp(jnp.sum(lam_q1.astype(F32) * lam_k1.astype(F32)))
           - jnp.exp(jnp.sum(lam_q2.astype(F32) * lam_k2.astype(F32))) + lam_init)
    yb = _sweep(lambda s, n: _diff_block(s, n, bq, b_keys, b_vals, lam, qpos, kpos), n_new)
    yb = _rmsnorm(yb, diff_g) * (1.0 - lam_init)

    cq = _rope(cq.reshape(bsz, n_new, H_DSA, HEAD_DIM), qpos)
    ck = _rope(ck.reshape(bsz, n_new, H_DSA, HEAD_DIM), qpos)
    cv = cv.reshape(bsz, n_new, H_DSA, HEAD_DIM)
    iq = _rope(iq.reshape(bsz, n_new, H_IDX, D_IDX), qpos)
    ik = _rope(ik, qpos)
    c_keys = jnp.concatenate([pc_k, ck], axis=1)
    c_vals = jnp.concatenate([pc_v, cv], axis=1)
    i_keys = jnp.concatenate([pc_i, ik], axis=1)
    k_sel = min(TOPK_MAX, n_keys // 4)
    yc = _sweep(lambda s, n: _dsa_block(s, n, cq, c_keys, c_vals, iq, i_keys, iw, qpos, kpos, k_sel), n_new)

    gates = jax.nn.sigmoid(gl.reshape(bsz, n_new, N_BRANCH, D_MODEL).astype(F32) + b_gate.astype(F32)).astype(x.dtype)
    merged = (gates[:, :, 0] * (ya.reshape(bsz, n_new, W_FOX) @ w_br_a)
              + gates[:, :, 1] * (yb.reshape(bsz, n_new, W_DIFF) @ w_br_b)
              + gates[:, :, 2] * (yc.reshape(bsz, n_new, W_DSA) @ w_br_c))
    x = _layernorm(DEEPNORM_ALPHA * x + merged @ w_o, ln1_g, ln1_b)
    ff = (jax.nn.silu(x @ w_ff1) * (x @ w_ff3)) @ w_ff2
    x = _layernorm(DEEPNORM_ALPHA * x + ff, ln2_g, ln2_b)
    return x, (ak, av, logf.astype(x.dtype), bk, bv, ck, cv, ik)


def setup_inputs(seed: int = 0) -> dict:
    key = jax.random.key(seed)
    ks = jax.random.split(key, 32)
    nrm = lambda k, shape, scale=1.0: jax.random.normal(k, shape, F32) * scale
    L = (DEPTH,)
    cb = (DEPTH, DEC_BATCH, PAST_LEN)
    return {
        'x_prompt': nrm(ks[0], (BATCH, SEQ, D_MODEL)),
        'x_sample': nrm(ks[1], (DEC_BATCH, DEC_SEQ, D_MODEL)),
        'cache_a_k': nrm(ks[2], cb + (H_FOX, HEAD_DIM)),
        'cache_a_v': nrm(ks[3], cb + (H_FOX, HEAD_DIM)),
        'cache_a_logf': jax.nn.log_sigmoid(2.0 + nrm(ks[4], cb + (H_FOX,))),
        'cache_b_k': nrm(ks[5], cb + (H_DIFF, 2, HEAD_DIM)),
        'cache_b_v': nrm(ks[6], cb + (H_DIFF, 2 * HEAD_DIM)),
        'cache_c_k': nrm(ks[7], cb + (H_DSA, HEAD_DIM)),
        'cache_c_v': nrm(ks[8], cb + (H_DSA, HEAD_DIM)),
        'cache_c_idx': nrm(ks[9], cb + (D_IDX,)),
        'w_in': nrm(ks[10], L + (D_MODEL, N_IN), D_MODEL ** -0.5),
        'b_fgate': 2.0 + nrm(ks[11], L + (H_FOX,), 0.1),
        'b_gate': nrm(ks[12], L + (N_BRANCH, D_MODEL), 0.01),
        'lam_q1': nrm(ks[13], L + (HEAD_DIM,), 0.1),
        'lam_k1': nrm(ks[14], L + (HEAD_DIM,), 0.1),
        'lam_q2': nrm(ks[15], L + (HEAD_DIM,), 0.1),
        'lam_k2': nrm(ks[16], L + (HEAD_DIM,), 0.1),
        'diff_norm_g': 1.0 + nrm(ks[17], L + (2 * HEAD_DIM,), 0.01),
        'w_br_a': nrm(ks[18], L + (W_FOX, D_MODEL), W_FOX ** -0.5),
        'w_br_b': nrm(ks[19], L + (W_DIFF, D_MODEL), W_DIFF ** -0.5),
        'w_br_c': nrm(ks[20], L + (W_DSA, D_MODEL), W_DSA ** -0.5),
        'w_o': nrm(ks[21], L + (D_MODEL, D_MODEL), D_MODEL ** -0.5 * DEEPNORM_BETA),
        'ln1_g': 1.0 + nrm(ks[22], L + (D_MODEL,), 0.01),
        'ln1_b': nrm(ks[23], L + (D_MODEL,), 0.01),
        'ln2_g': 1.0 + nrm(ks[24], L + (D_MODEL,), 0.01),
        'ln2_b': nrm(ks[25], L + (D_MODEL,), 0.01),
        'w_ff1': nrm(ks[26], L + (D_MODEL, D_FF), D_MODEL ** -0.5),
        'w_ff3': nrm(ks[27], L + (D_MODEL, D_FF), D_MODEL ** -0.5),
        'w_ff2': nrm(ks[28], L + (D_FF, D_MODEL), D_FF ** -0.5 * DEEPNORM_BETA),
    }


def reference(x_prompt, x_sample, cache_a_k, cache_a_v, cache_a_logf, cache_b_k, cache_b_v,
              cache_c_k, cache_c_v, cache_c_idx, w_in, b_fgate, b_gate, lam_q1, lam_k1, lam_q2,
              lam_k2, diff_norm_g, w_br_a, w_br_b, w_br_c, w_o, ln1_g, ln1_b, ln2_g, ln2_b,
              w_ff1, w_ff3, w_ff2):
    params = (w_in, b_fgate, b_gate, lam_q1, lam_k1, lam_q2, lam_k2, diff_norm_g, w_br_a, w_br_b,
              w_br_c, w_o, ln1_g, ln1_b, ln2_g, ln2_b, w_ff1, w_ff3, w_ff2)
    caches = (cache_a_k, cache_a_v, cache_a_logf, cache_b_k, cache_b_v, cache_c_k, cache_c_v, cache_c_idx)
    past_len = cache_a_k.shape[2]
    qpos_p = jnp.arange(x_prompt.shape[1], dtype=jnp.int32)
    qpos_s = past_len + jnp.arange(x_sample.shape[1], dtype=jnp.int32)
    yp, ys = x_prompt, x_sample
    rows_p, rows_s = [], []
    for l in range(DEPTH):
        lp = tuple(w[l] for w in params)
        yp, rp = _layer(yp, qpos_p, _empty_past(yp.shape[0], yp.dtype), lp, l)
        ys, rs = _layer(ys, qpos_s, tuple(c[l] for c in caches), lp, l)
        rows_p.append(rp)
        rows_s.append(rs)
    p_a_k, p_a_v, p_a_logf, p_b_k, p_b_v, p_c_k, p_c_v, p_c_idx = [jnp.stack([r[i] for r in rows_p]) for i in range(8)]
    s_a_k, s_a_v, s_a_logf, s_b_k, s_b_v, s_c_k, s_c_v, s_c_idx = [jnp.stack([r[i] for r in rows_s]) for i in range(8)]
    return (yp, ys, p_a_k, p_a_v, p_a_logf, p_b_k, p_b_v, p_c_k, p_c_v, p_c_idx,
            s_a_k, s_a_v, s_a_logf, s_b_k, s_b_v, s_c_k, s_c_v, s_c_idx)
```

```python
import os
import math
import contextlib
import numpy as np
import ml_dtypes
import concourse.bass as bass
import concourse.mybir as mybir
from concourse.bass_utils import run_bass_kernel_spmd

F32 = mybir.dt.float32
BF16 = mybir.dt.bfloat16
U32 = mybir.dt.uint32
ALU = mybir.AluOpType
AF = mybir.ActivationFunctionType
AX = mybir.AxisListType

D = 1024
TP = 4096
NS = 4
TS = 64
PAST = 1024
DEPTH = 4
NIN = 8272
DFF = 2816
NTOK = TP + NS * TS
NKS = PAST + TS
NKEY = TP + NS * NKS
KOFF = [0] + [TP + NKS * j for j in range(NS)]
ALPHA = (2 * DEPTH) ** 0.25
EPS = 1e-5
NEG = -30000.0
NBIS = 16
TOPK = 256

GROUPS = [("aq", 0, 512), ("ak", 512, 512), ("av", 1024, 512), ("af", 1536, 8),
          ("bq", 1544, 512), ("bk", 2056, 512), ("bv", 2568, 512),
          ("cq", 3080, 512), ("ck", 3592, 512), ("cv", 4104, 512),
          ("iq", 4616, 512), ("ikw", 5128, 72)] + [("gl%d" % i, 5200 + 512 * i, 512) for i in range(6)]


class Tok:
    __slots__ = ("sem", "val", "eng")

    def __init__(self, sem, val, eng):
        self.sem, self.val, self.eng = sem, val, eng


class Buf:
    __slots__ = ("w", "r", "name", "excl")

    def __init__(self, name="", excl=False):
        self.w = None
        self.r = {}
        self.name = name
        self.excl = excl


class Trk:
    def __init__(self, nc, es, ndq=14):
        self.nc = nc
        self.eng = {"pe": nc.tensor, "act": nc.scalar, "dve": nc.vector, "pool": nc.gpsimd, "sp": nc.sync}
        self.sem = {k: es.enter_context(nc.semaphore("s_" + k)) for k in ["pe", "act", "dve", "pool"]}
        self.cnt = {k: 0 for k in self.sem}
        self.seen = {k: {} for k in self.eng}
        self.dq = {q: [es.enter_context(nc.semaphore("d_%s%d" % (q, i))) for i in range(ndq)] for q in ["sp", "pool", "act"]}
        self.dqi = {q: 0 for q in self.dq}
        self.dqv = {}
        for q in self.dq:
            for s in self.dq[q]:
                self.dqv[s.num] = (s, 0)
        self.bar = es.enter_context(nc.semaphore("s_bar"))
        self.barcnt = 0
        self.ninst = 0

    def wait(self, e, tok):
        if tok is None:
            return
        k = tok.sem.num
        if self.seen[e].get(k, 0) >= tok.val:
            return
        self.eng[e].wait_ge(tok.sem, tok.val)
        self.seen[e][k] = tok.val
        self.ninst += 1

    def _deps(self, e, R, W):
        for b in R:
            t = b.w
            if t is not None and not (t.eng == e and e == "pe"):
                self.wait(e, t)
            if b.excl:
                for t in b.r.values():
                    if t.eng != e:
                        self.wait(e, t)
        for b in W:
            t = b.w
            if t is not None and t.eng != e:
                self.wait(e, t)
            for t in b.r.values():
                if t.eng != e:
                    self.wait(e, t)

    def _mark(self, tok, R, W):
        for b in R:
            b.r[tok.sem.num] = tok
        for b in W:
            b.w = tok
            b.r = {}

    def op(self, e, fn, R=(), W=()):
        self._deps(e, R, W)
        ins = fn(self.eng[e])
        self.cnt[e] += 1
        ins.then_inc(self.sem[e], 1)
        tok = Tok(self.sem[e], self.cnt[e], e)
        self._mark(tok, R, W)
        self.ninst += 1
        return tok

    def dma(self, q, out, in_, R=(), W=()):
        if q == "pool" and out.dtype == in_.dtype and os.environ.get("KPOOL2SP", "0") == "1":
            q = "sp"
        for b in R:
            self.wait(q, b.w)
        for b in W:
            self.wait(q, b.w)
            for t in b.r.values():
                self.wait(q, t)
        i = self.dqi[q]
        self.dqi[q] += 1
        s = self.dq[q][i % len(self.dq[q])]
        _, v = self.dqv[s.num]
        if v > 0:
            self.wait(q, Tok(s, v, None))
        self.eng[q].dma_start(out=out, in_=in_).then_inc(s, 16)
        v += 16
        self.dqv[s.num] = (s, v)
        tok = Tok(s, v, None)
        self._mark(tok, R, W)
        self.ninst += 1
        return tok

    def barrier(self):
        sp = "sp"
        for e in self.sem:
            if self.cnt[e] > 0:
                self.wait(sp, Tok(self.sem[e], self.cnt[e], e))
        for k, (s, v) in self.dqv.items():
            if v > 0:
                self.wait(sp, Tok(s, v, None))
        self.barcnt += 1
        self.eng[sp].sem_inc(self.bar, 1)
        for e in ["pe", "act", "dve", "pool"]:
            self.eng[e].wait_ge(self.bar, self.barcnt)
        for e in self.eng:
            for e2 in self.sem:
                self.seen[e][self.sem[e2].num] = self.cnt[e2]
            for k, (s, v) in self.dqv.items():
                self.seen[e][k] = v


class Pool_:
    def __init__(self, tiles, excl=False):
        self.t = tiles
        self.b = [Buf(excl=excl) for _ in tiles]
        self.i = 0

    def get(self):
        i = self.i % len(self.t)
        self.i += 1
        return self.t[i], self.b[i]


def build_program(depth=DEPTH):
    nc = bass.Bass("TRN2", target_bir_lowering=False)
    es = contextlib.ExitStack()
    with es:
        _build(nc, es, depth)
    return nc


def _build(nc, es, depth):
    def din(name, shape, dt=F32):
        return nc.dram_tensor(name, list(shape), dt, kind="ExternalInput").ap()

    def dout(name, shape, dt=F32):
        return nc.dram_tensor(name, list(shape), dt, kind="ExternalOutput").ap()

    def dscr(name, shape, dt):
        return nc.dram_tensor(name, list(shape), dt, kind="Internal").ap()

    x_in = din("x_in", [NTOK, D])
    c_ak = din("c_ak", [DEPTH, NS, PAST, 512])
    c_av = din("c_av", [DEPTH, NS, PAST, 512])
    c_lf = din("c_lf", [DEPTH, NS, PAST, 8])
    c_bk = din("c_bk", [DEPTH, NS, PAST, 512])
    c_bv = din("c_bv", [DEPTH, NS, PAST, 512])
    c_ck = din("c_ck", [DEPTH, NS, PAST, 512])
    c_cv = din("c_cv", [DEPTH, NS, PAST, 512])
    c_ix = din("c_ix", [DEPTH, NS, PAST, 64])
    w_in = din("w_in", [DEPTH, D, NIN])
    b_fgate = din("b_fgate", [DEPTH, 8])
    b_gate = din("b_gate", [DEPTH, 3 * D])
    lamv = din("lamv", [DEPTH, 4, 64])
    diff_g = din("diff_g", [DEPTH, 128])
    w_br = din("w_br", [DEPTH, 3, 512, D])
    w_o = din("w_o", [DEPTH, D, D])
    ln_gb = din("ln_gb", [DEPTH, 4, D])
    w_ff1 = din("w_ff1", [DEPTH, D, DFF])
    w_ff3 = din("w_ff3", [DEPTH, D, DFF])
    w_ff2 = din("w_ff2", [DEPTH, DFF, D])
    k_rope = din("k_rope", [NTOK, 128])
    k_maskA = din("k_maskA", [128, 4 * 512], BF16)
    k_maskB = din("k_maskB", [128, 4 * 512], BF16)
    k_ident = din("k_ident", [128, 128], BF16)
    k_tri = din("k_tri", [128, 128])
    k_pow2 = din("k_pow2", [128, NBIS + 1])

    y_out = dout("y_out", [NTOK, D])
    o_ak = dout("o_ak", [DEPTH, NTOK, 512])
    o_av = dout("o_av", [DEPTH, NTOK, 512])
    o_lf = dout("o_lf", [DEPTH, NTOK, 8])
    o_bk = dout("o_bk", [DEPTH, NTOK, 512])
    o_bv = dout("o_bv", [DEPTH, NTOK, 512])
    o_ck = dout("o_ck", [DEPTH, NTOK, 512])
    o_cv = dout("o_cv", [DEPTH, NTOK, 512])
    o_ix = dout("o_ix", [DEPTH, NTOK, 64])

    wb_in = dscr("wb_in", [DEPTH, D, NIN], BF16)
    wb_br = dscr("wb_br", [DEPTH, 3, 512, D], BF16)
    wb_o = dscr("wb_o", [DEPTH, D, D], BF16)
    wb_ff1 = dscr("wb_ff1", [DEPTH, D, DFF], BF16)
    wb_ff3 = dscr("wb_ff3", [DEPTH, D, DFF], BF16)
    wb_ff2 = dscr("wb_ff2", [DEPTH, DFF, D], BF16)
    xres = dscr("xres", [NTOK, D], F32)
    QT = {m: dscr("QT" + m, [512, NTOK], BF16) for m in "abci"}
    KT = {m: dscr("KT" + m, [512, NKEY], BF16) for m in "abc"}
    KTi = dscr("KTi", [64, NKEY], BF16)
    VV = {m: dscr("V" + m, [NKEY, 768], BF16) for m in "abc"}
    QAUG = dscr("QAUG", [8, 6, NTOK], BF16)
    KAUG = dscr("KAUG", [8, 6, NKEY], BF16)
    IW = dscr("IW", [NTOK, 8], F32)
    GATES = dscr("GATES", [NTOK, 3 * D], F32)
    YT = (dout if os.environ.get("KDBG", "0") == "1" else dscr)("YT", [3, 512, NTOK], BF16)
    MASKP = dscr("MASKP", [32, 128, TP], BF16)
    MASKS = (dout if os.environ.get("KDBG", "0") == "1" else dscr)("MASKS", [NS, 9, 128, TS], BF16)

    T = Trk(nc, es)

    uid = [0]

    def sb(stk, name, shape, dt):
        uid[0] += 1
        return stk.enter_context(nc.sbuf_tensor("%s_u%d" % (name, uid[0]), list(shape), dt))

    def ps(stk, name, shape, dt):
        uid[0] += 1
        return stk.enter_context(nc.psum_tensor("%s_u%d" % (name, uid[0]), list(shape), dt))

    ident = sb(es, "ident", [128, 128], BF16)
    tri = sb(es, "tri", [128, 128], F32)
    onesf = sb(es, "onesf", [128, 128], F32)
    maskA = sb(es, "maskA", [128, 4, 512], BF16)
    maskB = sb(es, "maskB", [128, 4, 512], BF16)
    pow2 = sb(es, "pow2", [128, NBIS + 1], F32)
    c256 = sb(es, "c256", [128, 1], F32)
    bgate_bc = sb(es, "bgate_bc", [128, 3 * D], F32)
    bfg_bc = sb(es, "bfg_bc", [128, 8], F32)
    lngb = sb(es, "lngb", [128, 4, D], F32)
    lam_in = sb(es, "lam_in", [128, 4, 64], F32)
    lam_t = sb(es, "lam_t", [128, 8], F32)
    dg = sb(es, "dg", [128, 2], F32)
    logf_all = sb(es, "logf_all", [128, NTOK // 128, 8], F32)
    B_const = Buf("const")
    B_layer = Buf("layerconst")
    B_logf = Buf("logf")

    T.dma("sp", ident[:], k_ident, W=[B_const])
    T.dma("sp", tri[:], k_tri, W=[B_const])
    T.dma("sp", maskA[:].rearrange("p j q -> p (j q)"), k_maskA, W=[B_const])
    T.dma("sp", maskB[:].rearrange("p j q -> p (j q)"), k_maskB, W=[B_const])
    T.dma("sp", pow2[:], k_pow2, W=[B_const])
    T.op("dve", lambda e: e.memset(onesf[:], 1.0), W=[B_const])
    T.op("dve", lambda e: e.memset(c256[:], float(TOPK)), W=[B_const])
    c_eps = sb(es, "c_eps", [128, 1], F32)
    c_eps128 = sb(es, "c_eps128", [128, 1], F32)
    T.op("dve", lambda e: e.memset(c_eps[:], EPS), W=[B_const])
    T.op("dve", lambda e: e.memset(c_eps128[:], 128.0 * EPS), W=[B_const])

    def cast_w(l):
        for r in range(0, D, 256):
            T.dma("pool", wb_in[l, r:r + 256, :], w_in[l, r:r + 256, :])
        T.dma("pool", wb_br[l].rearrange("a k n -> (a k) n"), w_br[l].rearrange("a k n -> (a k) n"))
        T.dma("pool", wb_o[l], w_o[l])
        T.dma("pool", wb_ff1[l], w_ff1[l])
        T.dma("pool", wb_ff3[l], w_ff3[l])
        T.dma("pool", wb_ff2[l], w_ff2[l])

    for l in range(depth):
        cast_w(l)
    T.barrier()
    if os.environ.get("KSTOP", "") == "cast":
        return

    def phase_A(l):
        stk = contextlib.ExitStack()
        with stk:
            xf = Pool_([sb(stk, "xf%d" % i, [128, D], F32) for i in range(2)])
            xb = Pool_([sb(stk, "xb%d" % i, [128, D], BF16) for i in range(2)])
            xT = Pool_([sb(stk, "xT%d" % i, [128, 8, 512], BF16) for i in range(2)])
            wt = Pool_([sb(stk, "wt%d" % i, [128, 8, 512], BF16) for i in range(3)])
            tokf = Pool_([sb(stk, "tokf%d" % i, [128, 512], F32) for i in range(4)])
            tokb = Pool_([sb(stk, "tokb%d" % i, [128, 512], BF16) for i in range(4)])
            tmpA = Pool_([sb(stk, "tmpA%d" % i, [128, 512], F32) for i in range(2)])
            tmpB = Pool_([sb(stk, "tmpB%d" % i, [128, 512], F32) for i in range(2)])
            tstage = Pool_([sb(stk, "tstage%d" % i, [128, 4, 512], BF16) for i in range(3)])
            vst_t = [sb(stk, "vstage%d" % i, [128, 768], BF16) for i in range(3)]
            vstage = Pool_(vst_t)
            gstage = Pool_([sb(stk, "gstage%d" % i, [128, 512], F32) for i in range(3)])
            cs = Pool_([sb(stk, "cs%d" % i, [128, 128], F32) for i in range(5)])
            sm = Pool_([sb(stk, "sm%d" % i, [128, 8], F32) for i in range(6)])
            pm = Pool_([ps(stk, "pm%d" % i, [128, 512], F32) for i in range(5)], excl=True)
            pt = Pool_([ps(stk, "pt%d" % i, [128, 8, 128], BF16) for i in range(3)], excl=True)

            for i, (v, b) in enumerate(zip(vstage.t, vstage.b)):
                T.op("dve", lambda e, v=v: e.memset(v[:], 1.0), W=[b])

            T.dma("sp", bgate_bc[:], b_gate[l].partition_broadcast(128), W=[B_layer])
            T.dma("sp", bfg_bc[:], b_fgate[l].partition_broadcast(128), W=[B_layer])

            xsrc = x_in if l == 0 else xres

            pending = []

            def transpose_to_stage(tb, tbB, st, stB, s, nblk=4, ncols=128):
                pending.append(lambda: _transpose_to_stage(tb, tbB, st, stB, s, nblk, ncols))

            def _transpose_to_stage(tb, tbB, st, stB, s, nblk=4, ncols=128):
                p, pB = pt.get()
                for j in range(nblk):
                    T.op("pe", lambda e, j=j: e.transpose(out=p[0:ncols, j, :], in_=tb[:, j * ncols:(j + 1) * ncols], identity=ident[:]),
                         R=[tbB, B_const], W=[pB])
                T.op("dve", lambda e: e.tensor_copy(out=st[0:ncols, 0:nblk, s * 128:(s + 1) * 128], in_=p[0:ncols, 0:nblk, :]), R=[pB], W=[stB])

            def stage_out_q(st, stB, dst, tok0, ntok):
                T.dma("pool", dst.rearrange("(b p) k -> p b k", p=128)[:, :, tok0:tok0 + ntok], st[:, :, 0:ntok], R=[stB])

            def stage_out_k(st, stB, dst, tile, nsub):
                if tile < 8:
                    T.dma("pool", dst.rearrange("(b p) k -> p b k", p=128)[:, :, tile * 512:(tile + 1) * 512], st[:, :, :], R=[stB])
                else:
                    for j in range(NS):
                        T.dma("pool", dst.rearrange("(b p) k -> p b k", p=128)[:, :, KOFF[j + 1] + PAST:KOFF[j + 1] + PAST + TS],
                              st[:, :, j * 64:(j + 1) * 64], R=[stB])

            def kslots(tile, s):
                if tile < 8:
                    t0 = tile * 512 + s * 128
                    return [(0, 128, t0)]
                return [(0, 64, KOFF[2 * s + 1] + PAST), (64, 128, KOFF[2 * s + 2] + PAST)]

            def rope(p, pB, csT, csB, nh, dst, dstB):
                w = nh * 64
                ta, taB = tmpA.get()
                tb_, tbB = tmpB.get()
                pv = p[:, 0:w].rearrange("p (h t d) -> p h t d", h=nh, t=2)
                tav = ta[:, 0:w].rearrange("p (h t d) -> p h t d", h=nh, t=2)
                tbv = tb_[:, 0:w].rearrange("p (h t d) -> p h t d", h=nh, t=2)
                cos2 = csT[:, 0:64].rearrange("p (t d) -> p t d", t=2).unsqueeze(1).to_broadcast([128, nh, 2, 32])
                nsin = csT[:, 64:96].unsqueeze(1).to_broadcast([128, nh, 32])
                psin = csT[:, 96:128].unsqueeze(1).to_broadcast([128, nh, 32])
                T.op("dve", lambda e: e.tensor_tensor(out=tav, in0=pv, in1=cos2, op=ALU.mult), R=[pB, csB], W=[taB])
                T.op("dve", lambda e: e.tensor_tensor(out=tbv[:, :, 0, :], in0=pv[:, :, 1, :], in1=nsin, op=ALU.mult), R=[pB, csB], W=[tbB])
                T.op("dve", lambda e: e.tensor_tensor(out=tbv[:, :, 1, :], in0=pv[:, :, 0, :], in1=psin, op=ALU.mult), R=[pB, csB], W=[tbB])
                T.op("pool", lambda e: e.tensor_tensor(out=dst[:, 0:w], in0=ta[:, 0:w], in1=tb_[:, 0:w], op=ALU.add), R=[taB, tbB], W=[dstB])

            ntiles = int(os.environ.get("KTILES", "9"))
            for tile in range(ntiles):
                nsub = 4 if tile < 8 else 2
                tok0 = tile * 512
                ntk = nsub * 128
                xTt, xTB = xT.get()
                cst = []
                for s in range(nsub):
                    xt, xtB = xf.get()
                    T.dma("sp", xt[:], xsrc[tok0 + s * 128: tok0 + (s + 1) * 128, :], W=[xtB])
                    xbt, xbB = xb.get()
                    T.op("act", lambda e: e.activation(out=xbt[:], in_=xt[:], func=AF.Copy), R=[xtB], W=[xbB])
                    for half in range(2):
                        p, pB = pt.get()
                        for j in range(4):
                            c = half * 4 + j
                            T.op("pe", lambda e, j=j, c=c: e.transpose(out=p[:, j, :], in_=xbt[:, c * 128:(c + 1) * 128], identity=ident[:]),
                                 R=[xbB, B_const], W=[pB])
                        T.op("dve", lambda e: e.tensor_copy(out=xTt[:, half * 4:(half + 1) * 4, s * 128:(s + 1) * 128], in_=p[:, 0:4, :]), R=[pB], W=[xTB])
                    c_, cB = cs.get()
                    T.dma("sp", c_[:], k_rope[tok0 + s * 128: tok0 + (s + 1) * 128, :], W=[cB])
                    cst.append((c_, cB))

                for (gname, c0, ncols) in GROUPS[:int(os.environ.get("KGROUPS", "99"))]:
                    wtt, wtB = wt.get()
                    T.dma("sp", wtt[:, :, 0:ncols], wb_in[l, :, c0:c0 + ncols].rearrange("(kc p) n -> p kc n", p=128), W=[wtB])
                    st = stB = None
                    if gname in ("aq", "ak", "bq", "bk", "cq", "ck", "iq", "ikw"):
                        st, stB = tstage.get()
                    for s in range(nsub):
                        p, pB = pm.get()
                        for kc in range(8):
                            T.op("pe", lambda e, kc=kc: e.matmul(p[:, 0:ncols], lhsT=xTt[:, kc, s * 128:(s + 1) * 128], rhs=wtt[:, kc, 0:ncols],
                                                                start=(kc == 0), stop=(kc == 7)), R=[xTB, wtB], W=[pB])
                        while len(pending) > 2:
                            pending.pop(0)()
                        g0 = tok0 + s * 128
                        csT, csB = cst[s]
                        kind = gname[0:2]
                        if gname == "aq":
                            tb, tbB = tokb.get()
                            T.op("act", lambda e: e.activation(out=tb[:], in_=p[:], func=AF.Identity, scale=0.125), R=[pB], W=[tbB])
                            transpose_to_stage(tb, tbB, st, stB, s)
                        elif gname == "ak":
                            tf, tfB = tokf.get()
                            T.op("act", lambda e: e.activation(out=tf[:], in_=p[:], func=AF.Copy), R=[pB], W=[tfB])
                            T.dma("pool", o_ak[l, g0:g0 + 128, :], tf[:], R=[tfB])
                            tb, tbB = tokb.get()
                            T.op("dve", lambda e: e.tensor_copy(out=tb[:], in_=p[:]), R=[pB], W=[tbB])
                            transpose_to_stage(tb, tbB, st, stB, s)
                        elif gname in ("av", "bv", "cv"):
                            tf, tfB = tokf.get()
                            T.op("act", lambda e: e.activation(out=tf[:], in_=p[:], func=AF.Copy), R=[pB], W=[tfB])
                            od = {"av": o_av, "bv": o_bv, "cv": o_cv}[gname]
                            T.dma("pool", od[l, g0:g0 + 128, :], tf[:], R=[tfB])
                            vs, vsB = vstage.get()
                            vsv = vs[:].rearrange("p (a c) -> p a c", c=192)
                            T.op("dve", lambda e: e.tensor_copy(out=vsv[:, :, 0:64], in_=p[:].rearrange("p (a t d) -> p a t d", a=4, t=2)[:, :, 0, :]), R=[pB], W=[vsB])
                            T.op("dve", lambda e: e.tensor_copy(out=vsv[:, :, 128:192], in_=p[:].rearrange("p (a t d) -> p a t d", a=4, t=2)[:, :, 1, :]), R=[pB], W=[vsB])
                            vd = VV[gname[0]]
                            for (p0, p1, r0) in kslots(tile, s):
                                T.dma("pool", vd[r0:r0 + (p1 - p0), :], vs[p0:p1, :], R=[vsB])
                        elif gname == "af":
                            z, zB = sm.get()
                            T.op("dve", lambda e: e.tensor_tensor(out=z[:], in0=p[:, 0:8], in1=bfg_bc[:], op=ALU.add), R=[pB, B_layer], W=[zB])
                            e1, e1B = sm.get()
                            T.op("act", lambda e: e.activation(out=e1[:], in_=z[:], func=AF.Exp, scale=-1.0), R=[zB], W=[e1B])
                            e2, e2B = sm.get()
                            T.op("act", lambda e: e.activation(out=e2[:], in_=e1[:], func=AF.Ln, bias=1.0), R=[e1B], W=[e2B])
                            gi = g0 // 128
                            T.op("dve", lambda e: e.tensor_scalar(out=logf_all[:, gi, :], in0=e2[:], scalar1=-1.0, scalar2=None, op0=ALU.mult), R=[e2B], W=[B_logf])
                            T.dma("pool", o_lf[l, g0:g0 + 128, :], logf_all[:, gi, :], R=[B_logf])
                        elif gname in ("bq", "cq", "iq"):
                            tf, tfB = tokf.get()
                            rope(p, pB, csT, csB, 8, tf, tfB)
                            tb, tbB = tokb.get()
                            T.op("act", lambda e: e.activation(out=tb[:], in_=tf[:], func=AF.Identity, scale=0.125), R=[tfB], W=[tbB])
                            transpose_to_stage(tb, tbB, st, stB, s)
                        elif gname in ("bk", "ck"):
                            tf, tfB = tokf.get()
                            rope(p, pB, csT, csB, 8, tf, tfB)
                            od = {"bk": o_bk, "ck": o_ck}[gname]
                            T.dma("pool", od[l, g0:g0 + 128, :], tf[:], R=[tfB])
                            tb, tbB = tokb.get()
                            T.op("act", lambda e: e.activation(out=tb[:], in_=tf[:], func=AF.Copy), R=[tfB], W=[tbB])
                            transpose_to_stage(tb, tbB, st, stB, s)
                        elif gname == "ikw":
                            tf, tfB = tokf.get()
                            rope(p, pB, csT, csB, 1, tf, tfB)
                            T.dma("pool", o_ix[l, g0:g0 + 128, :], tf[:, 0:64], R=[tfB])
                            tb, tbB = tokb.get()
                            T.op("act", lambda e: e.activation(out=tb[:, 0:64], in_=tf[:, 0:64], func=AF.Copy), R=[tfB], W=[tbB])
                            transpose_to_stage(tb, tbB, st, stB, s, nblk=1, ncols=64)
                            iwt, iwB = sm.get()
                            T.op("dve", lambda e: e.tensor_scalar(out=iwt[:], in0=p[:, 64:72], scalar1=8.0 ** -0.5, scalar2=None, op0=ALU.mult), R=[pB], W=[iwB])
                            T.dma("pool", IW[g0:g0 + 128, :], iwt[:], R=[iwB])
                        else:
                            gi = int(gname[2:])
                            tf, tfB = tokf.get()
                            T.op("dve", lambda e: e.tensor_tensor(out=tf[:], in0=p[:], in1=bgate_bc[:, gi * 512:(gi + 1) * 512], op=ALU.add), R=[pB, B_layer], W=[tfB])
                            gs, gsB = gstage.get()
                            T.op("act", lambda e: e.activation(out=gs[:], in_=tf[:], func=AF.Sigmoid), R=[tfB], W=[gsB])
                            T.dma("pool", GATES[g0:g0 + 128, gi * 512:(gi + 1) * 512], gs[:], R=[gsB])
                    def stage_out(gname=gname, st=st, stB=stB, tok0=tok0, ntk=ntk, tile=tile, nsub=nsub):
                        if gname in ("aq", "bq", "cq", "iq"):
                            stage_out_q(st, stB, QT[gname[0]], tok0, ntk)
                        elif gname in ("ak", "bk", "ck"):
                            stage_out_k(st, stB, KT[gname[0]], tile, nsub)
                        elif gname == "ikw":
                            if tile < 8:
                                T.dma("pool", KTi[:, tok0:tok0 + 512], st[0:64, 0, :], R=[stB])
                            else:
                                for j in range(NS):
                                    T.dma("pool", KTi[:, KOFF[j + 1] + PAST:KOFF[j + 1] + PAST + TS], st[0:64, 0, j * 64:(j + 1) * 64], R=[stB])
                    if st is not None:
                        pending.append(stage_out)
            while pending:
                pending.pop(0)()
        T.barrier()

    def phase_B0(l):
        stk = contextlib.ExitStack()
        with stk:
            cb = Pool_([sb(stk, "cb%d" % i, [128, 8, 512], BF16) for i in range(2)])
            cst = Pool_([sb(stk, "cst%d" % i, [128, 4, PAST], BF16) for i in range(2)])
            cvs_t = [sb(stk, "cvs%d" % i, [128, 8, 768], BF16) for i in range(2)]
            cvs = Pool_(cvs_t)
            pt = Pool_([ps(stk, "pt%d" % i, [128, 8, 128], BF16) for i in range(3)], excl=True)
            pc = Pool_([ps(stk, "pc%d" % i, [128, 512], F32) for i in range(2)], excl=True)
            lfp = sb(stk, "lfp", [128, 8, 8], F32)
            B_lfp = Buf()
            cumT = sb(stk, "cumT", [8, TP], F32)
            B_cum = Buf()
            hi = sb(stk, "hi", [8, TP], BF16)
            r1 = sb(stk, "r1", [8, TP], F32)
            aug = sb(stk, "aug", [8, 6, TP], BF16)
            B_aug = Buf()
            B_tmp = Buf()
            for v, b in zip(cvs.t, cvs.b):
                T.op("dve", lambda e, v=v: e.memset(v[:], 1.0), W=[b])

            for j in range(NS):
                k0 = KOFF[j + 1]
                for (src, dst, nblk) in ((c_ak, KT["a"], 4), (c_bk, KT["b"], 4), (c_ck, KT["c"], 4), (c_ix, KTi, 1)):
                    ncol = 512 if nblk == 4 else 64
                    c, cB = cb.get()
                    T.dma("pool", c[:, :, 0:ncol], src[l, j].rearrange("(t p) c -> p t c", p=128), W=[cB])
                    st, stB = cst.get()
                    for t in range(8):
                        p, pB = pt.get()
                        nb = nblk
                        w = 128 if nblk == 4 else 64
                        for jj in range(nb):
                            T.op("pe", lambda e, jj=jj, t=t, w=w: e.transpose(out=p[0:w, jj, :], in_=c[:, t, jj * w:(jj + 1) * w], identity=ident[:]),
                                 R=[cB, B_const], W=[pB])
                        T.op("dve" if t % 2 == 0 else "act",
                             (lambda e, t=t, w=w, nb=nb: e.tensor_copy(out=st[0:w, 0:nb, t * 128:(t + 1) * 128], in_=p[0:w, 0:nb, :])) if t % 2 == 0 else
                             (lambda e, t=t, w=w, nb=nb: e.activation(out=st[0:w, 0:nb, t * 128:(t + 1) * 128], in_=p[0:w, 0:nb, :], func=AF.Copy)),
                             R=[pB], W=[stB])
                    if nblk == 4:
                        T.dma("pool", dst.rearrange("(b p) k -> p b k", p=128)[:, :, k0:k0 + PAST], st[:, :, :], R=[stB])
                    else:
                        T.dma("pool", dst[:, k0:k0 + PAST], st[0:64, 0, :], R=[stB])
                for (src, m) in ((c_av, "a"), (c_bv, "b"), (c_cv, "c")):
                    c, cB = cb.get()
                    T.dma("pool", c[:], src[l, j].rearrange("(t p) c -> p t c", p=128), W=[cB])
                    v, vB = cvs.get()
                    vv = v[:].rearrange("p t (a c) -> p t a c", c=192)
                    cv_ = c[:].rearrange("p t (a u d) -> p t a u d", a=4, u=2)
                    for t in range(8):
                        T.op("pool", lambda e, t=t: e.tensor_copy(out=vv[:, t, :, 0:64], in_=cv_[:, t, :, 0, :]), R=[cB], W=[vB])
                        T.op("pool", lambda e, t=t: e.tensor_copy(out=vv[:, t, :, 128:192], in_=cv_[:, t, :, 1, :]), R=[cB], W=[vB])
                    T.dma("pool", VV[m][k0:k0 + PAST, :].rearrange("(t p) c -> p t c", p=128), v[:], R=[vB])

            def cum_segment(tiles, ntot, q_lo, qdst0, kdst0):
                pos = 0
                prev = None
                for (lt, ltB, npart, poff) in tiles:
                    p, pB = pc.get()
                    T.op("pe", lambda e: e.matmul(p[0:8, 0:npart], lhsT=lt, rhs=tri[poff:poff + npart, poff:poff + npart], start=True, stop=True), R=[ltB, B_const], W=[pB])
                    if prev is None:
                        T.op("dve", lambda e: e.tensor_copy(out=cumT[:, pos:pos + npart], in_=p[0:8, 0:npart]), R=[pB], W=[B_cum])
                    else:
                        pv = prev
                        T.op("dve", lambda e: e.tensor_scalar(out=cumT[:, pos:pos + npart], in0=p[0:8, 0:npart], scalar1=cumT[:, pv:pv + 1], scalar2=None, op0=ALU.add),
                             R=[pB, B_cum], W=[B_cum])
                    pos += npart
                    prev = pos - 1
                n = ntot
                T.op("dve", lambda e: e.tensor_copy(out=hi[:, 0:n], in_=cumT[:, 0:n]), R=[B_cum], W=[B_tmp])
                T.op("dve", lambda e: e.tensor_copy(out=aug[:, 0, 0:n], in_=hi[:, 0:n]), R=[B_tmp], W=[B_aug])
                T.op("dve", lambda e: e.tensor_tensor(out=r1[:, 0:n], in0=cumT[:, 0:n], in1=hi[:, 0:n], op=ALU.subtract), R=[B_cum, B_tmp], W=[B_tmp])
                T.op("dve", lambda e: e.tensor_copy(out=aug[:, 1, 0:n], in_=r1[:, 0:n]), R=[B_tmp], W=[B_aug])
                T.op("dve", lambda e: e.tensor_tensor(out=r1[:, 0:n], in0=r1[:, 0:n], in1=aug[:, 1, 0:n], op=ALU.subtract), R=[B_tmp, B_aug], W=[B_tmp])
                T.op("dve", lambda e: e.tensor_copy(out=aug[:, 2, 0:n], in_=r1[:, 0:n]), R=[B_tmp], W=[B_aug])
                T.op("dve", lambda e: e.memset(aug[:, 3:6, 0:n], 1.0), W=[B_aug])
                nq = n - q_lo
                T.dma("pool", QAUG[:, :, qdst0:qdst0 + nq], aug[:, :, q_lo:n], R=[B_aug])
                T.op("dve", lambda e: e.tensor_scalar(out=aug[:, 3:6, 0:n], in0=aug[:, 0:3, 0:n], scalar1=-1.0, scalar2=None, op0=ALU.mult), R=[B_aug], W=[B_aug])
                T.op("dve", lambda e: e.memset(aug[:, 0:3, 0:n], 1.0), W=[B_aug])
                T.dma("pool", KAUG[:, :, kdst0:kdst0 + n], aug[:, :, 0:n], R=[B_aug])

            cum_segment([(logf_all[:, t, :], B_logf, 128, 0) for t in range(32)], TP, 0, 0, 0)
            for j in range(NS):
                T.dma("sp", lfp[:], c_lf[l, j].rearrange("(t p) h -> p t h", p=128), W=[B_lfp])
                tiles = [(lfp[:, t, :], B_lfp, 128, 0) for t in range(8)]
                sub = 32 + j // 2
                po = (j % 2) * 64
                tiles.append((logf_all[po:po + 64, sub, :], B_logf, 64, po))
                cum_segment(tiles, NKS, PAST, TP + j * TS, KOFF[j + 1])

            T.dma("sp", lam_in[:].rearrange("p a d -> p (a d)"), lamv[l].rearrange("a d -> (a d)").partition_broadcast(128), W=[B_layer])
            T.dma("sp", dg[:, 0:1], diff_g[l].rearrange("(p o) -> p o", o=1), W=[B_layer])
            lam_init = 0.8 - 0.6 * math.exp(-0.3 * l)
            B_l = Buf()
            T.op("dve", lambda e: e.tensor_tensor(out=lam_in[:, 0, :], in0=lam_in[:, 0, :], in1=lam_in[:, 1, :], op=ALU.mult), R=[B_layer], W=[B_l])
            T.op("dve", lambda e: e.tensor_tensor(out=lam_in[:, 2, :], in0=lam_in[:, 2, :], in1=lam_in[:, 3, :], op=ALU.mult), R=[B_layer], W=[B_l])
            T.op("dve", lambda e: e.tensor_reduce(out=lam_t[:, 0:1], in_=lam_in[:, 0, :], axis=AX.X, op=ALU.add), R=[B_l], W=[B_l])
            T.op("dve", lambda e: e.tensor_reduce(out=lam_t[:, 1:2], in_=lam_in[:, 2, :], axis=AX.X, op=ALU.add), R=[B_l], W=[B_l])
            T.op("act", lambda e: e.activation(out=lam_t[:, 2:4], in_=lam_t[:, 0:2], func=AF.Exp), R=[B_l], W=[B_l])
            T.op("dve", lambda e: e.tensor_tensor(out=lam_t[:, 4:5], in0=lam_t[:, 3:4], in1=lam_t[:, 2:3], op=ALU.subtract), R=[B_l], W=[B_l])
            T.op("dve", lambda e: e.tensor_scalar(out=lam_t[:, 5:6], in0=lam_t[:, 4:5], scalar1=-lam_init, scalar2=None, op0=ALU.add), R=[B_l], W=[B_l])
            T.op("dve", lambda e: e.tensor_scalar(out=dg[:, 1:2], in0=dg[:, 0:1], scalar1=(1.0 - lam_init) * math.sqrt(128.0), scalar2=None, op0=ALU.mult), R=[B_layer], W=[B_l])
        T.barrier()

    SEGS = [dict(q0=0, nq=TP, k0=0, nk=TP, prompt=True, j=-1)] + \
           [dict(q0=TP + j * TS, nq=TS, k0=KOFF[j + 1], nk=NKS, prompt=False, j=j) for j in range(NS)]

    def phase_B1(l):
        stk = contextlib.ExitStack()
        with stk:
            kti = sb(stk, "kti", [64, TP], BF16)
            B_kti = Buf()
            qti = Pool_([sb(stk, "qti%d" % i, [64, 8, 128], BF16) for i in range(2)])
            iwp = Pool_([sb(stk, "iwp%d" % i, [128, 8], F32) for i in range(2)])
            score = Pool_([sb(stk, "score%d" % i, [128, TP], F32) for i in range(2)])
            rtmp = Pool_([sb(stk, "rtmp%d" % i, [128, 512], F32) for i in range(4)])
            junk = sb(stk, "junk", [128, TP], F32)
            B_junk = Buf()
            mrow = Pool_([sb(stk, "mrow%d" % i, [128, TP], BF16) for i in range(2)])
            mstage = Pool_([sb(stk, "mstage%d" % i, [128, 32, 128], BF16) for i in range(2)])
            smp = Pool_([sb(stk, "smp%d" % i, [128, NBIS + 12], F32) for i in range(2)])
            ib = Pool_([ps(stk, "ib%d" % i, [128, 512], F32) for i in range(4)], excl=True)
            pt = Pool_([ps(stk, "ptm%d" % i, [128, 8, 128], BF16) for i in range(3)], excl=True)

            for seg in SEGS:
                q0, nq, k0, nk = seg["q0"], seg["nq"], seg["k0"], seg["nk"]
                T.dma("sp", kti[:, 0:nk], KTi[:, k0:k0 + nk], W=[B_kti])
                nqt = (nq + 127) // 128
                for qt in range(nqt):
                    nqq = min(128, nq - qt * 128)
                    L = 128 * (qt + 1) if seg["prompt"] else nk
                    tq0 = q0 + qt * 128
                    qi, qiB = qti.get()
                    T.dma("sp", qi[:, :, 0:nqq], QT["i"].rearrange("(h d) t -> d h t", d=64)[:, :, tq0:tq0 + nqq], W=[qiB])
                    iw, iwB = iwp.get()
                    T.dma("sp", iw[0:nqq, :], IW[tq0:tq0 + nqq, :], W=[iwB])
                    sc, scB = score.get()
                    for kg in range((L + 511) // 512):
                        kn = min(512, L - kg * 512)
                        for h in range(8):
                            p, pB = ib.get()
                            T.op("pe", lambda e: e.matmul(p[0:nqq, 0:kn], lhsT=qi[:, h, 0:nqq], rhs=kti[:, kg * 512:kg * 512 + kn], start=True, stop=True),
                                 R=[qiB, B_kti], W=[pB])
                            rt, rtB = rtmp.get()
                            T.op("act", lambda e: e.activation(out=rt[0:nqq, 0:kn], in_=p[0:nqq, 0:kn], func=AF.Relu), R=[pB], W=[rtB])
                            dst = sc[0:nqq, kg * 512:kg * 512 + kn]
                            if h == 0:
                                T.op("dve", lambda e: e.tensor_scalar(out=dst, in0=rt[0:nqq, 0:kn], scalar1=iw[0:nqq, 0:1], scalar2=None, op0=ALU.mult),
                                     R=[rtB, iwB], W=[scB])
                            else:
                                T.op("dve", lambda e: e.scalar_tensor_tensor(out=dst, in0=rt[0:nqq, 0:kn], scalar=iw[0:nqq, h:h + 1], in1=dst, op0=ALU.mult, op1=ALU.add),
                                     R=[rtB, iwB, scB], W=[scB])
                    sm, smB = smp.get()
                    WH = 8
                    scv = sc[0:nqq, 0:L]
                    if seg["prompt"]:
                        T.op("dve", lambda e: e.memset(sc[0:64, L - 64:L], 1e30), W=[scB])
                    T.op("dve", lambda e: e.tensor_reduce(out=sm[0:nqq, 0:1], in_=scv, axis=AX.X, op=ALU.min), R=[scB], W=[smB])
                    if seg["prompt"]:
                        T.op("dve", lambda e: e.memset(sc[0:64, L - 64:L], -1e30), W=[scB])
                    T.op("dve", lambda e: e.tensor_reduce(out=sm[0:nqq, 1:2], in_=scv, axis=AX.X, op=ALU.max), R=[scB], W=[smB])
                    T.op("dve", lambda e: e.scalar_tensor_tensor(out=sm[0:nqq, 2:3], in0=sm[0:nqq, 1:2], scalar=1.0, in1=sm[0:nqq, 0:1], op0=ALU.add, op1=ALU.subtract),
                         R=[smB], W=[smB])
                    T.op("dve", lambda e: e.tensor_scalar(out=sm[0:nqq, WH:WH + NBIS + 1], in0=pow2[0:nqq, :], scalar1=sm[0:nqq, 2:3], scalar2=None, op0=ALU.mult),
                         R=[smB, B_const], W=[smB])
                    T.op("dve", lambda e: e.tensor_tensor(out=sm[0:nqq, 3:4], in0=sm[0:nqq, 0:1], in1=sm[0:nqq, WH:WH + 1], op=ALU.add), R=[smB], W=[smB])
                    for i in range(NBIS):
                        T.op("dve", lambda e: e.tensor_scalar(out=junk[0:nqq, 0:L], in0=scv, scalar1=sm[0:nqq, 3:4], scalar2=0.0, op0=ALU.is_ge, op1=ALU.add,
                                                              accum_out=sm[0:nqq, 4:5]), R=[scB, smB], W=[smB, B_junk])
                        T.op("dve", lambda e: e.tensor_scalar(out=sm[0:nqq, 5:6], in0=sm[0:nqq, 4:5], scalar1=c256[0:nqq, :], scalar2=sm[0:nqq, WH + i:WH + i + 1],
                                                              op0=ALU.is_ge, op1=ALU.mult), R=[smB, B_const], W=[smB])
                        T.op("dve", lambda e: e.scalar_tensor_tensor(out=sm[0:nqq, 3:4], in0=sm[0:nqq, 5:6], scalar=sm[0:nqq, WH + i + 1:WH + i + 2], in1=sm[0:nqq, 3:4],
                                                                     op0=ALU.subtract, op1=ALU.add), R=[smB], W=[smB])
                    T.op("dve", lambda e: e.scalar_tensor_tensor(out=sm[0:nqq, 6:7], in0=sm[0:nqq, WH + NBIS:WH + NBIS + 1], scalar=-2.0, in1=sm[0:nqq, 3:4], op0=ALU.mult, op1=ALU.add), R=[smB], W=[smB])
                    mr, mrB = mrow.get()
                    T.op("dve", lambda e: e.tensor_scalar(out=mr[0:nqq, 0:L], in0=scv, scalar1=sm[0:nqq, 6:7], scalar2=NEG, op0=ALU.is_lt, op1=ALU.mult),
                         R=[scB, smB], W=[mrB])
                    ms, msB = mstage.get()
                    if seg["prompt"]:
                        Lg = 512 * (qt // 4 + 1)
                        if Lg > L:
                            T.op("dve", lambda e: e.memset(mr[0:nqq, L:Lg], NEG), W=[mrB])
                            L = Lg
                    nkt = (L + 127) // 128
                    for kt4 in range(0, nkt, 4):
                        nb = min(4, nkt - kt4)
                        p, pB = pt.get()
                        kns = []
                        for jj in range(nb):
                            kt = kt4 + jj
                            kn = min(128, L - kt * 128)
                            kns.append(kn)
                            T.op("pe", lambda e: e.transpose(out=p[0:kn, jj, 0:nqq], in_=mr[0:nqq, kt * 128:kt * 128 + kn], identity=ident[0:nqq, 0:nqq]),
                                 R=[mrB, B_const], W=[pB])
                        nfull = sum(1 for k_ in kns if k_ == 128)
                        if nfull > 0:
                            T.op("act", lambda e: e.activation(out=ms[:, kt4:kt4 + nfull, 0:nqq], in_=p[:, 0:nfull, 0:nqq], func=AF.Copy), R=[pB], W=[msB])
                        if nfull < nb:
                            kn = kns[-1]
                            T.op("act", lambda e: e.activation(out=ms[0:kn, kt4 + nfull, 0:nqq], in_=p[0:kn, nfull, 0:nqq], func=AF.Copy), R=[pB], W=[msB])
                    if seg["prompt"]:
                        T.dma("pool", MASKP[0:nkt, :, tq0:tq0 + 128].rearrange("t k q -> k t q"), ms[:, 0:nkt, :], R=[msB])
                    else:
                        T.dma("pool", MASKS[seg["j"], 0:8, :, :].rearrange("t k q -> k t q"), ms[:, 0:8, 0:TS], R=[msB])
                        T.dma("pool", MASKS[seg["j"], 8, 0:TS, :], ms[0:TS, 8, 0:TS], R=[msB])
        T.barrier()

    def seg_blocks(seg, g):
        if seg["prompt"]:
            return [(kt * 128, 128, (kt - 4 * g) if kt >= 4 * g else None) for kt in range(4 * g + 4)]
        return [(kt * 128, 128, None) for kt in range(8)] + [(PAST, TS, 0)]

    def phase_B(l, mixers):
        stk = contextlib.ExitStack()
        with stk:
            ktile = Pool_([sb(stk, "ktile%d" % i, [128, TP], BF16) for i in range(4)])
            qtile = Pool_([sb(stk, "qtile%d" % i, [128, TP], BF16) for i in range(4)])
            vtile = Pool_([sb(stk, "vtile%d" % i, [128, 32, 192], BF16) for i in range(2)])
            pbuf = Pool_([sb(stk, "pbuf%d" % i, [128, 512], BF16) for i in range(4)])
            mtile = Pool_([sb(stk, "mtile%d" % i, [128, 4, 512], BF16) for i in range(3)])
            rec = Pool_([sb(stk, "rec%d" % i, [128, 512], F32) for i in range(4)])
            yf = Pool_([sb(stk, "yf%d" % i, [128, 512], F32) for i in range(4)])
            ysq = Pool_([sb(stk, "ysq%d" % i, [128, 512], F32) for i in range(2)])
            ystage = Pool_([sb(stk, "ystage%d" % i, [128, 512], BF16) for i in range(3)])
            sbank = Pool_([ps(stk, "sbank%d" % i, [128, 512], F32) for i in range(3)], excl=True)
            obank = Pool_([ps(stk, "obank%d" % i, [128, 512], F32) for i in range(4)], excl=True)
            xbank = Pool_([ps(stk, "xbank%d" % i, [128, 512], F32) for i in range(1)], excl=True)
            LAG = 2

            def epi_single(mi, h, g, qn, q0, ob, obB):
                hh = h % 2
                r0, d0 = hh * 64, (1 - hh) * 64
                rc, rcB = rec.get()
                T.op("dve", lambda e: e.reciprocal(out=rc[r0:r0 + 64, 0:qn], in_=ob[d0:d0 + 64, 0:qn]), R=[obB], W=[rcB])
                ys, ysB = ystage.get()
                T.op("dve", lambda e: e.tensor_tensor(out=ys[r0:r0 + 64, 0:qn], in0=ob[r0:r0 + 64, 0:qn], in1=rc[r0:r0 + 64, 0:qn], op=ALU.mult), R=[obB, rcB], W=[ysB])
                T.dma("pool", YT[mi, h * 64:(h + 1) * 64, q0 + g * 512:q0 + g * 512 + qn], ys[r0:r0 + 64, 0:qn], R=[ysB])

            def epi_diff(h, g, qn, q0, O):
                yb, ybB = yf.get()
                for half in range(2):
                    r0, d0 = half * 64, (1 - half) * 64
                    ts = []
                    for c in range(2):
                        ob, obB = O[c][half]
                        rc, rcB = rec.get()
                        T.op("dve", lambda e: e.reciprocal(out=rc[r0:r0 + 64, 0:qn], in_=ob[d0:d0 + 64, 0:qn]), R=[obB], W=[rcB])
                        t, tB = yf.get()
                        T.op("dve", lambda e: e.tensor_tensor(out=t[r0:r0 + 64, 0:qn], in0=ob[r0:r0 + 64, 0:qn], in1=rc[r0:r0 + 64, 0:qn], op=ALU.mult), R=[obB, rcB], W=[tB])
                        ts.append((t, tB))
                    T.op("dve", lambda e: e.scalar_tensor_tensor(out=yb[r0:r0 + 64, 0:qn], in0=ts[1][0][r0:r0 + 64, 0:qn], scalar=lam_t[r0:r0 + 64, 5:6],
                                                                 in1=ts[0][0][r0:r0 + 64, 0:qn], op0=ALU.mult, op1=ALU.add), R=[ts[0][1], ts[1][1], B_layer], W=[ybB])
                sq, sqB = ysq.get()
                T.op("act", lambda e: e.activation(out=sq[:, 0:qn], in_=yb[:, 0:qn], func=AF.Square), R=[ybB], W=[sqB])
                xb_, xbB = xbank.get()
                T.op("pe", lambda e: e.matmul(xb_[:, 0:qn], lhsT=onesf[:], rhs=sq[:, 0:qn], start=True, stop=True), R=[sqB, B_const], W=[xbB])
                rs, rsB = ysq.get()
                T.op("act", lambda e: e.activation(out=rs[:, 0:qn], in_=xb_[:, 0:qn], func=AF.Ln, bias=c_eps128[:, 0:1]), R=[xbB, B_const], W=[rsB])
                T.op("act", lambda e: e.activation(out=rs[:, 0:qn], in_=rs[:, 0:qn], func=AF.Exp, scale=-0.5), R=[rsB], W=[rsB])
                ys, ysB = ystage.get()
                T.op("dve", lambda e: e.scalar_tensor_tensor(out=ys[:, 0:qn], in0=yb[:, 0:qn], scalar=dg[:, 1:2], in1=rs[:, 0:qn], op0=ALU.mult, op1=ALU.mult),
                     R=[ybB, rsB, B_layer], W=[ysB])
                T.dma("pool", YT[1, h * 128:(h + 1) * 128, q0 + g * 512:q0 + g * 512 + qn], ys[:, 0:qn], R=[ysB])

            for mix in mixers:
                mi = "abc".index(mix)
                nunits = 4 if mix == "b" else 8
                for seg in SEGS:
                    q0, nq, k0, nk = seg["q0"], seg["nq"], seg["k0"], seg["nk"]
                    ngrp = (nq + 511) // 512
                    nkt = (nk + 127) // 128
                    ustate = [dict() for _ in range(nunits)]
                    vshared = {}

                    def load_unit(u):
                        st = ustate[u]
                        kq = []
                        maps = [2 * u, 2 * u + 1] if mix == "b" else [u]
                        for m in maps:
                            kt_, ktB = ktile.get()
                            T.dma("sp", kt_[0:64, 0:nk], KT[mix][m * 64:(m + 1) * 64, k0:k0 + nk], W=[ktB])
                            qt_, qtB = qtile.get()
                            T.dma("sp", qt_[0:64, 0:nq], QT[mix][m * 64:(m + 1) * 64, q0:q0 + nq], W=[qtB])
                            kk = 64
                            if mix == "a":
                                T.dma("sp", kt_[64:70, 0:nk], KAUG[u, :, k0:k0 + nk], W=[ktB])
                                T.dma("sp", qt_[64:70, 0:nq], QAUG[u, :, q0:q0 + nq], W=[qtB])
                                kk = 70
                            kq.append((kt_, ktB, qt_, qtB, kk))
                        st["kq"] = kq
                        pair = u if mix == "b" else u // 2
                        if pair not in vshared:
                            vt, vtB = vtile.get()
                            nfull = nk // 128
                            for t0 in range(0, nfull, 8):
                                t1 = min(nfull, t0 + 8)
                                T.dma("sp", vt[:, t0:t1, :], VV[mix][k0 + t0 * 128:k0 + t1 * 128, pair * 192:(pair + 1) * 192].rearrange("(t p) c -> p t c", p=128), W=[vtB])
                            rem = nk - nfull * 128
                            if rem > 0:
                                T.dma("sp", vt[0:rem, nfull, :], VV[mix][k0 + nfull * 128:k0 + nk, pair * 192:(pair + 1) * 192], W=[vtB])
                            vshared[pair] = (vt, vtB)
                        st["v"] = vshared[pair]

                    steps = []
                    for u in range(nunits):
                        for g in range(ngrp):
                            qn = min(512, nq - g * 512)
                            blocks = seg_blocks(seg, g)
                            nmaps = 2 if mix == "b" else 1
                            gstate = {}
                            for c in range(nmaps):
                                for bi, (ko, kn, dj) in enumerate(blocks):
                                    def emit_S(u=u, c=c, bi=bi, ko=ko, kn=kn, dj=dj, g=g, qn=qn, gstate=gstate, blocks=blocks):
                                        kt_, ktB, qt_, qtB, kk = ustate[u]["kq"][c]
                                        sbk, sbB = sbank.get()
                                        mk = None
                                        if mix == "c":
                                            if bi % 4 == 0:
                                                mt, mtB = mtile.get()
                                                nb = min(4, len(blocks) - bi)
                                                if seg["prompt"]:
                                                    T.dma("sp", mt[:, 0:nb, 0:qn], MASKP[bi:bi + nb, :, g * 512:g * 512 + qn].rearrange("t k q -> k t q"), W=[mtB])
                                                else:
                                                    T.dma("sp", mt[:, 0:nb, 0:qn], MASKS[seg["j"], bi:bi + nb, :, 0:qn].rearrange("t k q -> k t q"), W=[mtB])
                                                gstate["mt"] = (mt, mtB)
                                            mt, mtB = gstate["mt"]
                                            mk = (mt[0:kn, bi % 4, 0:qn], mtB)
                                        elif dj is not None:
                                            mtab = maskA if mix == "a" else maskB
                                            if seg["prompt"]:
                                                mk = (mtab[0:kn, dj, 0:qn], B_const)
                                            elif mix == "a":
                                                mk = (mtab[0:kn, 0, 0:qn], B_const)
                                        T.op("pe", lambda e: e.matmul(sbk[0:kn, 0:qn], lhsT=kt_[0:kk, ko:ko + kn], rhs=qt_[0:kk, g * 512:g * 512 + qn],
                                                                      start=True, stop=(mk is None)), R=[ktB, qtB], W=[sbB])
                                        if mk is not None:
                                            T.op("pe", lambda e: e.matmul(sbk[0:kn, 0:qn], lhsT=ident[0:kn, 0:kn], rhs=mk[0], start=False, stop=True),
                                                 R=[mk[1], B_const], W=[sbB])
                                        pb, pbB = pbuf.get()
                                        T.op("act", lambda e: e.activation(out=pb[0:kn, 0:qn], in_=sbk[0:kn, 0:qn], func=AF.Exp), R=[sbB], W=[pbB])
                                        gstate[(c, bi)] = (pb, pbB)

                                    def emit_PV(u=u, c=c, bi=bi, ko=ko, kn=kn, g=g, qn=qn, gstate=gstate, nblk=len(blocks), nmaps=nmaps):
                                        if g == 0 and c == 0 and bi == 0 and u + 1 < nunits:
                                            load_unit(u + 1)
                                        pb, pbB = gstate.pop((c, bi))
                                        vt, vtB = ustate[u]["v"]
                                        kt = ko // 128
                                        halves = [0, 1] if mix == "b" else [u % 2]
                                        if bi == 0:
                                            gstate[("o", c)] = [obank.get() for _ in halves]
                                        for hi_, hf in enumerate(halves):
                                            ob, obB = gstate[("o", c)][hi_]
                                            T.op("pe", lambda e: e.matmul(ob[:, 0:qn], lhsT=vt[0:kn, kt, hf * 64:hf * 64 + 128], rhs=pb[0:kn, 0:qn],
                                                                          start=(bi == 0), stop=(bi == nblk - 1)), R=[vtB, pbB], W=[obB])
                                        if bi == nblk - 1 and c == nmaps - 1:
                                            if mix == "b":
                                                epi_diff(u, g, qn, q0, [gstate[("o", 0)], gstate[("o", 1)]])
                                            else:
                                                ob, obB = gstate[("o", 0)][0]
                                                epi_single(mi, u, g, qn, q0, ob, obB)
                                    steps.append((emit_S, emit_PV))
                    load_unit(0)
                    for i in range(len(steps) + LAG):
                        if i < len(steps):
                            steps[i][0]()
                        if i >= LAG:
                            steps[i - LAG][1]()
        T.barrier()

    xmid = dscr("xmid", [NTOK, D], F32)
    X1T = dscr("X1T", [D, NTOK], BF16)

    def layernorm(stk_pools, src, srcB, gi, dst, dstB):
        stt, stB, mvp, tmpn = stk_pools
        st_, st_B = stt.get()
        for c in range(2):
            T.op("dve", lambda e: e.bn_stats(out=st_[:, c, :], in_=src[:, c * 512:(c + 1) * 512]), R=[srcB], W=[st_B])
        mv, mvB = mvp.get()
        T.op("dve", lambda e: e.bn_aggr(out=mv[:, 0:2], in_=st_[:].rearrange("p c s -> p (c s)")), R=[st_B], W=[mvB])
        T.op("act", lambda e: e.activation(out=mv[:, 3:4], in_=mv[:, 1:2], func=AF.Ln, bias=c_eps[:, 0:1]), R=[mvB, B_const], W=[mvB])
        T.op("act", lambda e: e.activation(out=mv[:, 2:3], in_=mv[:, 3:4], func=AF.Exp, scale=-0.5), R=[mvB], W=[mvB])
        tn, tnB = tmpn.get()
        T.op("dve", lambda e: e.tensor_scalar(out=tn[:], in0=src[:], scalar1=mv[:, 0:1], scalar2=mv[:, 2:3], op0=ALU.subtract, op1=ALU.mult), R=[srcB, mvB], W=[tnB])
        T.op("pool", lambda e: e.tensor_tensor(out=tn[:], in0=tn[:], in1=lngb[:, gi, :], op=ALU.mult), R=[tnB, B_layer], W=[tnB])
        T.op("pool", lambda e: e.tensor_tensor(out=dst[:], in0=tn[:], in1=lngb[:, gi + 1, :], op=ALU.add), R=[tnB, B_layer], W=[dstB])

    def phase_C1(l):
        stk = contextlib.ExitStack()
        with stk:
            wbr_t = sb(stk, "wbr_t", [128, 12, D], BF16)
            wo_t = sb(stk, "wo_t", [128, 8, D], BF16)
            B_w = Buf()
            ytile = Pool_([sb(stk, "ytile%d" % i, [128, 12, 512], BF16) for i in range(2)])
            gates = Pool_([sb(stk, "gates%d" % i, [128, 3 * D], F32) for i in range(2)])
            xin = Pool_([sb(stk, "xin%d" % i, [128, D], F32) for i in range(2)])
            mrg = Pool_([sb(stk, "mrg%d" % i, [128, D], F32) for i in range(2)])
            gt = Pool_([sb(stk, "gt%d" % i, [128, 512], F32) for i in range(3)])
            mb = Pool_([sb(stk, "mb%d" % i, [128, D], BF16) for i in range(2)])
            mT = Pool_([sb(stk, "mT%d" % i, [128, 8, 128], BF16) for i in range(2)])
            rr = Pool_([sb(stk, "rr%d" % i, [128, D], F32) for i in range(2)])
            x1 = Pool_([sb(stk, "x1_%d" % i, [128, D], F32) for i in range(2)])
            x1b = Pool_([sb(stk, "x1b%d" % i, [128, D], BF16) for i in range(2)])
            x1s = Pool_([sb(stk, "x1s%d" % i, [128, 8, 512], BF16) for i in range(2)])
            stt = Pool_([sb(stk, "stt%d" % i, [128, 2, 6], F32) for i in range(2)])
            mvp = Pool_([sb(stk, "mvp%d" % i, [128, 4], F32) for i in range(2)])
            tmpn = Pool_([sb(stk, "tmpn%d" % i, [128, D], F32) for i in range(2)])
            pm = Pool_([ps(stk, "pm%d" % i, [128, 512], F32) for i in range(5)], excl=True)
            pt = Pool_([ps(stk, "ptc%d" % i, [128, 8, 128], BF16) for i in range(3)], excl=True)

            T.dma("sp", wbr_t[:], wb_br[l].rearrange("a (c p) n -> p (a c) n", p=128), W=[B_w])
            T.dma("sp", wo_t[:], wb_o[l].rearrange("(c p) n -> p c n", p=128), W=[B_w])
            T.dma("sp", lngb[:].rearrange("p a d -> p (a d)"), ln_gb[l].rearrange("a d -> (a d)").partition_broadcast(128), W=[B_layer])
            xsrc = x_in if l == 0 else xres
            for tile in range(9):
                nsub = 4 if tile < 8 else 2
                tok0 = tile * 512
                ntk = nsub * 128
                yt, ytB = ytile.get()
                T.dma("sp", yt[:, :, 0:ntk], YT.rearrange("m (c p) t -> p (m c) t", p=128)[:, :, tok0:tok0 + ntk], W=[ytB])
                xs, xsB = x1s.get()
                for s in range(nsub):
                    g0 = tok0 + s * 128
                    ga, gaB = gates.get()
                    T.dma("sp", ga[:], GATES[g0:g0 + 128, :], W=[gaB])
                    xt, xtB = xin.get()
                    T.dma("sp", xt[:], xsrc[g0:g0 + 128, :], W=[xtB])
                    mg, mgB = mrg.get()
                    for half in range(2):
                        banks = []
                        for m in range(3):
                            p, pB = pm.get()
                            for c in range(4):
                                T.op("pe", lambda e: e.matmul(p[:], lhsT=yt[:, m * 4 + c, s * 128:(s + 1) * 128], rhs=wbr_t[:, m * 4 + c, half * 512:(half + 1) * 512],
                                                              start=(c == 0), stop=(c == 3)), R=[ytB, B_w], W=[pB])
                            banks.append((p, pB))
                        mgh = mg[:, half * 512:(half + 1) * 512]
                        T.op("dve", lambda e: e.tensor_tensor(out=mgh, in0=banks[0][0][:], in1=ga[:, half * 512:(half + 1) * 512], op=ALU.mult), R=[banks[0][1], gaB], W=[mgB])
                        for m in (1, 2):
                            t, tB = gt.get()
                            T.op("dve", lambda e: e.tensor_tensor(out=t[:], in0=banks[m][0][:], in1=ga[:, m * D + half * 512:m * D + (half + 1) * 512], op=ALU.mult),
                                 R=[banks[m][1], gaB], W=[tB])
                            T.op("pool", lambda e: e.tensor_tensor(out=mgh, in0=mgh, in1=t[:], op=ALU.add), R=[tB, mgB], W=[mgB])
                    mbt, mbB = mb.get()
                    T.op("act", lambda e: e.activation(out=mbt[:], in_=mg[:], func=AF.Copy), R=[mgB], W=[mbB])
                    mt_, mtB = mT.get()
                    for half in range(2):
                        p, pB = pt.get()
                        for j in range(4):
                            c = half * 4 + j
                            T.op("pe", lambda e: e.transpose(out=p[:, j, :], in_=mbt[:, c * 128:(c + 1) * 128], identity=ident[:]), R=[mbB, B_const], W=[pB])
                        T.op("dve", lambda e: e.tensor_copy(out=mt_[:, half * 4:(half + 1) * 4, :], in_=p[:, 0:4, :]), R=[pB], W=[mtB])
                    r, rB = rr.get()
                    for half in range(2):
                        p, pB = pm.get()
                        for c in range(8):
                            T.op("pe", lambda e: e.matmul(p[:], lhsT=mt_[:, c, :], rhs=wo_t[:, c, half * 512:(half + 1) * 512], start=(c == 0), stop=(c == 7)),
                                 R=[mtB, B_w], W=[pB])
                        T.op("dve", lambda e: e.scalar_tensor_tensor(out=r[:, half * 512:(half + 1) * 512], in0=xt[:, half * 512:(half + 1) * 512], scalar=ALPHA, in1=p[:],
                                                                     op0=ALU.mult, op1=ALU.add), R=[xtB, pB], W=[rB])
                    x1t, x1B = x1.get()
                    layernorm((stt, None, mvp, tmpn), r, rB, 0, x1t, x1B)
                    T.dma("pool", xmid[g0:g0 + 128, :], x1t[:], R=[x1B])
                    xb_, xbB = x1b.get()
                    T.op("act", lambda e: e.activation(out=xb_[:], in_=x1t[:], func=AF.Copy), R=[x1B], W=[xbB])
                    for half in range(2):
                        p, pB = pt.get()
                        for j in range(4):
                            c = half * 4 + j
                            T.op("pe", lambda e: e.transpose(out=p[:, j, :], in_=xb_[:, c * 128:(c + 1) * 128], identity=ident[:]), R=[xbB, B_const], W=[pB])
                        T.op("dve", lambda e: e.tensor_copy(out=xs[:, half * 4:(half + 1) * 4, s * 128:(s + 1) * 128], in_=p[:, 0:4, :]), R=[pB], W=[xsB])
                T.dma("pool", X1T.rearrange("(c p) t -> p c t", p=128)[:, :, tok0:tok0 + ntk], xs[:, :, 0:ntk], R=[xsB])
        T.barrier()

    def phase_C2(l, last):
        stk = contextlib.ExitStack()
        with stk:
            wf2_t = sb(stk, "wf2_t", [128, 22, D], BF16)
            B_w2 = Buf()
            xT = Pool_([sb(stk, "xTc%d" % i, [128, 8, 512], BF16) for i in range(2)])
            wf1 = Pool_([sb(stk, "wf1_%d" % i, [128, 8, 512], BF16) for i in range(2)])
            wf3 = Pool_([sb(stk, "wf3_%d" % i, [128, 8, 512], BF16) for i in range(2)])
            gT = sb(stk, "gT", [128, 22, 512], BF16)
            B_gT = [Buf() for _ in range(22)]
            sil = Pool_([sb(stk, "sil%d" % i, [128, 512], F32) for i in range(3)])
            x1 = Pool_([sb(stk, "x1c%d" % i, [128, D], F32) for i in range(2)])
            rr = Pool_([sb(stk, "rrc%d" % i, [128, D], F32) for i in range(2)])
            x2 = Pool_([sb(stk, "x2_%d" % i, [128, D], F32) for i in range(2)])
            stt = Pool_([sb(stk, "sttc%d" % i, [128, 2, 6], F32) for i in range(2)])
            mvp = Pool_([sb(stk, "mvpc%d" % i, [128, 4], F32) for i in range(2)])
            tmpn = Pool_([sb(stk, "tmpnc%d" % i, [128, D], F32) for i in range(2)])
            pa = Pool_([ps(stk, "pa%d" % i, [128, 512], F32) for i in range(3)], excl=True)
            pb_ = Pool_([ps(stk, "pb%d" % i, [128, 512], F32) for i in range(3)], excl=True)
            po = Pool_([ps(stk, "po%d" % i, [128, 512], F32) for i in range(2)], excl=True)

            T.dma("sp", wf2_t[:], wb_ff2[l].rearrange("(c p) n -> p c n", p=128), W=[B_w2])
            dst = y_out if last else xres
            for tile in range(9):
                nsub = 4 if tile < 8 else 2
                tok0 = tile * 512
                ntk = nsub * 128
                xt, xtB = xT.get()
                T.dma("sp", xt[:, :, 0:ntk], X1T.rearrange("(c p) t -> p c t", p=128)[:, :, tok0:tok0 + ntk], W=[xtB])
                for fg in range(6):
                    f0 = fg * 512
                    fn = min(512, DFF - f0)
                    w1, w1B = wf1.get()
                    T.dma("sp", w1[:, :, 0:fn], wb_ff1[l, :, f0:f0 + fn].rearrange("(c p) n -> p c n", p=128), W=[w1B])
                    w3, w3B = wf3.get()
                    T.dma("sp", w3[:, :, 0:fn], wb_ff3[l, :, f0:f0 + fn].rearrange("(c p) n -> p c n", p=128), W=[w3B])
                    for fc in range(fn // 128):
                        ch = fg * 4 + fc
                        a, aB = pa.get()
                        for c in range(8):
                            T.op("pe", lambda e: e.matmul(a[:, 0:ntk], lhsT=w1[:, c, fc * 128:(fc + 1) * 128], rhs=xt[:, c, 0:ntk], start=(c == 0), stop=(c == 7)),
                                 R=[w1B, xtB], W=[aB])
                        b, bB = pb_.get()
                        for c in range(8):
                            T.op("pe", lambda e: e.matmul(b[:, 0:ntk], lhsT=w3[:, c, fc * 128:(fc + 1) * 128], rhs=xt[:, c, 0:ntk], start=(c == 0), stop=(c == 7)),
                                 R=[w3B, xtB], W=[bB])
                        sl, slB = sil.get()
                        T.op("act", lambda e: e.activation(out=sl[:, 0:ntk], in_=a[:, 0:ntk], func=AF.Silu), R=[aB], W=[slB])
                        T.op("dve", lambda e: e.tensor_tensor(out=gT[:, ch, 0:ntk], in0=b[:, 0:ntk], in1=sl[:, 0:ntk], op=ALU.mult), R=[bB, slB], W=[B_gT[ch]])
                for s in range(nsub):
                    g0 = tok0 + s * 128
                    x1t, x1B = x1.get()
                    T.dma("sp", x1t[:], xmid[g0:g0 + 128, :], W=[x1B])
                    r, rB = rr.get()
                    for half in range(2):
                        p, pB = po.get()
                        for ch in range(22):
                            T.op("pe", lambda e: e.matmul(p[:], lhsT=gT[:, ch, s * 128:(s + 1) * 128], rhs=wf2_t[:, ch, half * 512:(half + 1) * 512],
                                                          start=(ch == 0), stop=(ch == 21)), R=[B_gT[ch], B_w2], W=[pB])
                        T.op("dve", lambda e: e.scalar_tensor_tensor(out=r[:, half * 512:(half + 1) * 512], in0=x1t[:, half * 512:(half + 1) * 512], scalar=ALPHA, in1=p[:],
                                                                     op0=ALU.mult, op1=ALU.add), R=[x1B, pB], W=[rB])
                    x2t, x2B = x2.get()
                    layernorm((stt, None, mvp, tmpn), r, rB, 2, x2t, x2B)
                    T.dma("pool", dst[g0:g0 + 128, :], x2t[:], R=[x2B])
        T.barrier()

    stop = os.environ.get("KSTOP", "")
    for l in range(depth):
        phase_A(l)
        if stop == "A":
            break
        phase_B0(l)
        if stop == "B0":
            break
        phase_B1(l)
        if stop == "B1":
            break
        phase_B(l, ("a", "b", "c"))
        if stop == "B":
            break
        phase_C1(l)
        if stop == "C1":
            break
        phase_C2(l, l == depth - 1)
    T.barrier()
    print("kernel build: instructions incl. waits ~", T.ninst, "cnt", T.cnt)


def _consts():
    half = 32
    inv = (10000.0 ** (-np.arange(half, dtype=np.float32) / half)).astype(np.float32)
    pos = np.concatenate([np.arange(TP), np.tile(PAST + np.arange(TS), NS)]).astype(np.float32)
    ang = pos[:, None] * inv[None, :]
    cos = np.cos(ang).astype(np.float32)
    sin = np.sin(ang).astype(np.float32)
    rope = np.concatenate([cos, cos, -sin, sin], axis=1).astype(np.float32)
    k = np.arange(128)[:, None, None]
    j = np.arange(4)[None, :, None]
    q = np.arange(512)[None, None, :]
    mA = np.where(128 * j + k <= q, 0.0, NEG).astype(np.float32).reshape(128, 2048)
    mB = np.where((128 * j + k) // 64 <= q // 64, 0.0, NEG).astype(np.float32).reshape(128, 2048)
    ident = np.eye(128, dtype=np.float32)
    tri = (np.arange(128)[:, None] <= np.arange(128)[None, :]).astype(np.float32)
    pow2 = np.tile((2.0 ** -(np.arange(NBIS + 1) + 1.0)).astype(np.float32)[None, :], (128, 1))
    bf = ml_dtypes.bfloat16
    return dict(k_rope=rope, k_maskA=mA.astype(bf), k_maskB=mB.astype(bf), k_ident=ident.astype(bf), k_tri=tri, k_pow2=pow2)


_PROG = {}


def kernel(x_prompt, x_sample, cache_a_k, cache_a_v, cache_a_logf, cache_b_k, cache_b_v,
           cache_c_k, cache_c_v, cache_c_idx, w_in, b_fgate, b_gate, lam_q1, lam_k1, lam_q2,
           lam_k2, diff_norm_g, w_br_a, w_br_b, w_br_c, w_o, ln1_g, ln1_b, ln2_g, ln2_b,
           w_ff1, w_ff3, w_ff2):
    depth = int(os.environ.get("KDEPTH", DEPTH))
    f = lambda a: np.ascontiguousarray(np.asarray(a, dtype=np.float32))
    if depth not in _PROG:
        _PROG[depth] = build_program(depth)
    nc = _PROG[depth]
    consts = _consts()
    shared = dict(
        w_in=f(w_in), b_fgate=f(b_fgate), b_gate=f(b_gate).reshape(DEPTH, 3 * D),
        lamv=np.ascontiguousarray(np.stack([f(lam_q1), f(lam_k1), f(lam_q2), f(lam_k2)], axis=1)),
        diff_g=f(diff_norm_g),
        w_br=np.ascontiguousarray(np.stack([f(w_br_a), f(w_br_b), f(w_br_c)], axis=1)),
        w_o=f(w_o),
        ln_gb=np.ascontiguousarray(np.stack([f(ln1_g), f(ln1_b), f(ln2_g), f(ln2_b)], axis=1)),
        w_ff1=f(w_ff1), w_ff3=f(w_ff3), w_ff2=f(w_ff2), **consts)
    xp, xs = f(x_prompt), f(x_sample)
    caches = dict(c_ak=f(cache_a_k), c_av=f(cache_a_v), c_lf=f(cache_a_logf), c_bk=f(cache_b_k), c_bv=f(cache_b_v),
                  c_ck=f(cache_c_k), c_cv=f(cache_c_v), c_ix=f(cache_c_idx))
    in_maps = []
    for b in range(8):
        m = dict(shared)
        m["x_in"] = np.ascontiguousarray(np.concatenate([xp[b], xs[NS * b:NS * (b + 1)].reshape(NS * TS, D)], axis=0))
        for k_, v in caches.items():
            w = int(np.prod(v.shape[3:]))
            m[k_] = np.ascontiguousarray(v[:, NS * b:NS * (b + 1)].reshape(DEPTH, NS, PAST, w))
        in_maps.append(m)
    res = run_bass_kernel_spmd(nc, in_maps, core_ids=list(range(8)))
    R = res.results
    yp = np.stack([R[b]["y_out"][:TP] for b in range(8)])
    ys = np.concatenate([R[b]["y_out"][TP:].reshape(NS, TS, D) for b in range(8)])
    outs_p, outs_s = [], []
    shapes = [("o_ak", (8, 64)), ("o_av", (8, 64)), ("o_lf", (8,)), ("o_bk", (4, 2, 64)), ("o_bv", (4, 128)),
              ("o_ck", (8, 64)), ("o_cv", (8, 64)), ("o_ix", (64,))]
    for name, tail in shapes:
        p = np.stack([R[b][name][:, :TP] for b in range(8)], axis=1)
        s = np.concatenate([R[b][name][:, TP:].reshape(DEPTH, NS, TS, -1) for b in range(8)], axis=1)
        outs_p.append(np.ascontiguousarray(p.reshape((DEPTH, 8, TP) + tail)).astype(np.float32))
        outs_s.append(np.ascontiguousarray(s.reshape((DEPTH, 8 * NS, TS) + tail)).astype(np.float32))
    return (yp.astype(np.float32), ys.astype(np.float32), *outs_p, *outs_s)
```
